# Optimizing a Trainium2 kernel written in Bass

```python
import jax, jax.numpy as jnp
from jax import lax
import numpy as np

D_MODEL = 2048
BATCH = 4
SEQ = 8192
DEPTH = 1

MEM_LEN = 256
HEAD_DIM = 128
H_HGRN = 6
H_FOX = 6
H_MEM = 4
W_HGRN = H_HGRN * HEAD_DIM
W_FOX = H_FOX * HEAD_DIM
W_MEM = H_MEM * HEAD_DIM
D_FF = 128 * ((8 * D_MODEL // 3 + 127) // 128)
CHUNK = 64
Q_BLOCK = 128
N_BRANCH = 3
EPS = 1e-6
IN_SIZES = (W_HGRN, W_HGRN, W_HGRN, W_HGRN,
            W_FOX, W_FOX, W_FOX, H_FOX,
            W_MEM)
IN_COLS = sum(IN_SIZES)
IN_SPLITS = tuple(int(s) for s in np.cumsum(IN_SIZES)[:-1])

kernel_name = "hybrid_hgrn2_fox_mem_macaron"


def rms_norm(x, g):
    xf = x.astype(jnp.float32)
    y = xf * lax.rsqrt(jnp.mean(xf * xf, axis=-1, keepdims=True) + EPS)
    return (y * g.astype(jnp.float32)).astype(x.dtype)


def swiglu(x, w_gate, w_up, w_down):
    return (jax.nn.silu(x @ w_gate) * (x @ w_up)) @ w_down


def split_heads(a, n):
    b, t, _ = a.shape
    return a.reshape(b, t, n, -1).transpose(0, 2, 1, 3)


def merge_heads(a):
    b, h, t, d = a.shape
    return a.transpose(0, 2, 1, 3).reshape(b, t, h * d)


def hgrn2_chunk_scan(q, k, v, log_f):
    bsz, h, t, dk = q.shape
    dv = v.shape[-1]
    n = t // CHUNK

    def to_chunks(a):
        return jnp.moveaxis(a.reshape(bsz, h, n, CHUNK, a.shape[-1]), 2, 0)

    qc, kc, vc, gc = to_chunks(q), to_chunks(k), to_chunks(v), to_chunks(log_f)
    causal = jnp.tril(jnp.ones((CHUNK, CHUNK), dtype=bool))[:, :, None]

    def step(state, inp):
        qb, kb, vb, gb = inp
        bcum = jnp.cumsum(gb, axis=2)
        o_inter = jnp.einsum('bhtk,bhkv->bhtv', qb * jnp.exp(bcum), state)
        diff = bcum[:, :, :, None, :] - bcum[:, :, None, :, :]
        decay = jnp.exp(jnp.where(causal, diff, -jnp.inf))
        scores = jnp.einsum('bhtk,bhtsk->bhts', qb, decay * kb[:, :, None, :, :])
        o_intra = jnp.einsum('bhts,bhsv->bhtv', scores, vb)
        last = bcum[:, :, -1:, :]
        k_to_end = kb * jnp.exp(last - bcum)
        new_state = jnp.exp(last[:, :, 0, :])[..., None] * state + \
            jnp.einsum('bhsk,bhsv->bhkv', k_to_end, vb)
        return new_state, o_inter + o_intra

    s0 = jnp.zeros((bsz, h, dk, dv), jnp.float32)
    _, o = lax.scan(step, s0, (qc, kc, vc, gc))
    return jnp.moveaxis(o, 0, 2).reshape(bsz, h, t, dv)


def forgetting_attention(q, k, v, log_f):
    bsz, h, t, d = q.shape
    c = jnp.cumsum(log_f, axis=-1)
    nb = t // Q_BLOCK
    q_blocks = jnp.moveaxis(q.reshape(bsz, h, nb, Q_BLOCK, d), 2, 0)
    c_blocks = jnp.moveaxis(c.reshape(bsz, h, nb, Q_BLOCK), 2, 0)
    starts = jnp.arange(nb, dtype=jnp.int32) * Q_BLOCK
    key_pos = jnp.arange(t, dtype=jnp.int32)
    scale = d ** -0.5

    def block(args):
        q_blk, c_blk, start = args
        s = jnp.einsum('bhqd,bhkd->bhqk', q_blk, k).astype(jnp.float32) * scale
        s = s + c_blk[..., None] - c[:, :, None, :]
        q_pos = start + jnp.arange(Q_BLOCK, dtype=jnp.int32)
        s = jnp.where(key_pos[None, :] <= q_pos[:, None], s, -jnp.inf)
        p = jax.nn.softmax(s, axis=-1).astype(v.dtype)
        return jnp.einsum('bhqk,bhkd->bhqd', p, v)

    o = lax.map(block, (q_blocks, c_blocks, starts))
    return jnp.moveaxis(o, 0, 2).reshape(bsz, h, t, d)


def memory_attention(q, mem_k, mem_v):
    s = jnp.einsum('bhtd,bhmd->bhtm', q, mem_k).astype(jnp.float32) * (HEAD_DIM ** -0.5)
    p = jax.nn.softmax(s, axis=-1).astype(mem_v.dtype)
    return jnp.einsum('bhtm,bhmd->bhtd', p, mem_v)


def hybrid_mixer(u, mem, mem_g, w_in, lb, hgrn_g, fox_b, w_mem_kv,
                 w_hgrn_out, w_fox_out, w_mem_out, w_gate, w_o):
    bsz, t, _ = u.shape
    proj = u @ w_in
    (hq, hf, hi, hog, fq, fk, fv, ff, mq) = jnp.split(proj, IN_SPLITS, axis=-1)

    lbf = lb.astype(jnp.float32)
    g = lbf + (1.0 - lbf) * jax.nn.sigmoid(hf.astype(jnp.float32))
    log_g = jnp.log(g)
    o_h = hgrn2_chunk_scan(split_heads(jax.nn.silu(hq).astype(jnp.float32), H_HGRN),
                           split_heads(1.0 - g, H_HGRN),
                           split_heads(hi.astype(jnp.float32), H_HGRN),
                           split_heads(log_g, H_HGRN))
    o_h = o_h * lax.rsqrt(jnp.mean(o_h * o_h, axis=-1, keepdims=True) + EPS)
    o_h = o_h * hgrn_g.astype(jnp.float32).reshape(H_HGRN, HEAD_DIM)[None, :, None, :]
    o_h = merge_heads(o_h).astype(u.dtype) * jax.nn.silu(hog)

    fox_logf = jax.nn.log_sigmoid(ff.astype(jnp.float32) + fox_b.astype(jnp.float32))
    o_f = forgetting_attention(split_heads(fq, H_FOX), split_heads(fk, H_FOX),
                               split_heads(fv, H_FOX), fox_logf.transpose(0, 2, 1))
    o_f = merge_heads(o_f)

    mem_kv = rms_norm(mem, mem_g) @ w_mem_kv
    mk, mv = jnp.split(mem_kv, 2, axis=-1)
    o_m = merge_heads(memory_attention(split_heads(mq, H_MEM),
                                       split_heads(mk, H_MEM), split_heads(mv, H_MEM)))

    gates = jax.nn.sigmoid(u @ w_gate).reshape(bsz, t, N_BRANCH, D_MODEL)
    merged = gates[:, :, 0] * (o_h @ w_hgrn_out) + \
        gates[:, :, 1] * (o_f @ w_fox_out) + \
        gates[:, :, 2] * (o_m @ w_mem_out)
    return merged @ w_o


def setup_inputs(seed: int = 0) -> dict:
    key = jax.random.key(seed)
    ks = jax.random.split(key, 32)
    L = DEPTH

    def dense(k, fan_in, fan_out):
        return jax.random.normal(k, (L, fan_in, fan_out), jnp.float32) * fan_in ** -0.5

    def gain(k, n):
        return 1.0 + 0.05 * jax.random.normal(k, (L, n), jnp.float32)

    return {
        "x": jax.random.normal(ks[0], (BATCH, SEQ, D_MODEL), jnp.float32),
        "mem": jax.random.normal(ks[1], (BATCH, MEM_LEN, D_MODEL), jnp.float32),
        "ffn1_pre": gain(ks[2], D_MODEL),
        "ffn1_post": gain(ks[3], D_MODEL),
        "ffn1_wg": dense(ks[4], D_MODEL, D_FF),
        "ffn1_wu": dense(ks[5], D_MODEL, D_FF),
        "ffn1_wd": dense(ks[6], D_FF, D_MODEL),
        "mix_pre": gain(ks[7], D_MODEL),
        "mix_post": gain(ks[8], D_MODEL),
        "mem_norm": gain(ks[9], D_MODEL),
        "w_in": dense(ks[10], D_MODEL, IN_COLS),
        "hgrn_lb": 0.1 * jax.random.normal(ks[11], (DEPTH + 1, W_HGRN), jnp.float32),
        "hgrn_gnorm": gain(ks[12], W_HGRN),
        "fox_fb": 1.0 + 0.1 * jax.random.normal(ks[13], (L, H_FOX), jnp.float32),
        "w_mem_kv": dense(ks[14], D_MODEL, 2 * W_MEM),
        "w_hgrn_out": dense(ks[15], W_HGRN, D_MODEL),
        "w_fox_out": dense(ks[16], W_FOX, D_MODEL),
        "w_mem_out": dense(ks[17], W_MEM, D_MODEL),
        "w_gate": dense(ks[18], D_MODEL, N_BRANCH * D_MODEL),
        "w_o": dense(ks[19], D_MODEL, D_MODEL),
        "ffn2_pre": gain(ks[20], D_MODEL),
        "ffn2_post": gain(ks[21], D_MODEL),
        "ffn2_wg": dense(ks[22], D_MODEL, D_FF),
        "ffn2_wu": dense(ks[23], D_MODEL, D_FF),
        "ffn2_wd": dense(ks[24], D_FF, D_MODEL),
    }


def reference(x, mem, ffn1_pre, ffn1_post, ffn1_wg, ffn1_wu, ffn1_wd,
              mix_pre, mix_post, mem_norm, w_in, hgrn_lb, hgrn_gnorm, fox_fb,
              w_mem_kv, w_hgrn_out, w_fox_out, w_mem_out, w_gate, w_o,
              ffn2_pre, ffn2_post, ffn2_wg, ffn2_wu, ffn2_wd):
    lb_all = jnp.cumsum(jax.nn.softmax(hgrn_lb.astype(jnp.float32), axis=0), axis=0)
    for l in range(DEPTH):
        h = swiglu(rms_norm(x, ffn1_pre[l]), ffn1_wg[l], ffn1_wu[l], ffn1_wd[l])
        x = x + 0.5 * rms_norm(h, ffn1_post[l])
        m = hybrid_mixer(rms_norm(x, mix_pre[l]), mem, mem_norm[l], w_in[l], lb_all[l],
                         hgrn_gnorm[l], fox_fb[l], w_mem_kv[l], w_hgrn_out[l],
                         w_fox_out[l], w_mem_out[l], w_gate[l], w_o[l])
        x = x + rms_norm(m, mix_post[l])
        h = swiglu(rms_norm(x, ffn2_pre[l]), ffn2_wg[l], ffn2_wu[l], ffn2_wd[l])
        x = x + 0.5 * rms_norm(h, ffn2_post[l])
    return x
```

```python
import numpy as np
from contextlib import ExitStack
import concourse.bass as bass
import concourse.mybir as mybir
from concourse.bass_utils import run_bass_kernel_spmd

F32 = mybir.dt.float32
BF16 = mybir.dt.bfloat16
AF = mybir.ActivationFunctionType
ALU = mybir.AluOpType
AX = mybir.AxisListType
EPS = 1e-6


class Cfg:
    def __init__(self, D=2048, T=8192, NB=4, DFF=5504, HH=3, HF=3, HM=2, ML=256, TB=512):
        self.D, self.T, self.NB, self.DFF = D, T, NB, DFF
        self.HH, self.HF, self.HM, self.ML, self.TB = HH, HF, HM, ML, TB
        self.TO = T // 2
        self.KD = D // 128
        self.KF = DFF // 128
        self.NT = TB // 128
        self.NBLK = self.TO // TB
        self.NBLK2 = T // TB
        self.NTT = T // 128
        self.NCOL = (4 * HH + 3 * HF + HM) * 128
        self.OW = (HH + HF + HM) * 128
        self.NH2 = 2 * (HH + HF + HM)
        self.QW = min(512, D)
        self.NQ = D // self.QW
        self.NQP = max(1, min(self.NQ, 8 // self.NT))
        self.NPASS = self.NQ // self.NQP
        self.ncores = 2 * NB


class T_:
    __slots__ = ("ap", "w", "r", "name", "excl")

    def __init__(self, ap, name="", excl=False):
        self.excl = excl
        self.ap = ap
        self.w = None
        self.r = []
        self.name = name

    def __getitem__(self, idx):
        return self.ap[idx]


class KB:
    SEM_ROT = 30000

    def __init__(self, nc, n_dma_sems=16):
        self.nc = nc
        self.eng = {"pe": nc.tensor, "dve": nc.vector, "act": nc.scalar,
                    "pool": nc.gpsimd, "sp": nc.sync}
        self.sem, self.cnt, self.rot = {}, {}, {}
        for e in self.eng:
            self.sem[e] = nc.alloc_semaphore(name=f"s_{e}_0")
            self.cnt[e] = 0
            self.rot[e] = 0
        self.seen = {e: {} for e in self.eng}
        self.dsems = [nc.alloc_semaphore(name=f"dma{i}") for i in range(2 * n_dma_sems)]
        self.dcnt = [0] * (2 * n_dma_sems)
        self.dnext = {"sp": 0, "pool": 0, "act": 0}
        self.nd = n_dma_sems
        self.ccsem = nc.alloc_semaphore(name="ccsem")
        self.cccnt = 0
        import os
        self.limit = int(os.environ.get("KB_LIMIT", "1000000000"))
        self.nops = 0
        self.log = []
        self.pend = {}

    def _deps(self, reads, writes):
        deps = []
        for t in reads:
            if t.w is not None:
                deps.append(t.w)
        for t in writes:
            if t.w is not None:
                deps.append(t.w)
            deps.extend(t.r)
        return deps

    def _wait(self, e, deps, skip_self=False):
        best = {}
        for (s, v) in deps:
            if skip_self and s is self.sem[e]:
                continue
            k = id(s)
            if k not in best or best[k][1] < v:
                best[k] = (s, v)
        seen = self.seen[e]
        for k, (s, v) in best.items():
            if seen.get(k, 0) >= v:
                continue
            self.eng[e].wait_ge(s, v)
            seen[k] = v

    def _record(self, tok, reads, writes):
        for t in writes:
            t.w = tok
            t.r = []
        for t in reads:
            t.r.append(tok)
            if len(t.r) > 8:
                best = {}
                for (s, v) in t.r:
                    k = id(s)
                    if k not in best or best[k][1] < v:
                        best[k] = (s, v)
                t.r = list(best.values())

    def op(self, e, fn, reads=(), writes=(), inc=True):
        self.nops += 1
        if self.nops > self.limit:
            return None
        writes = list(writes) + [t for t in reads if t.excl]
        reads = [t for t in reads if not t.excl]
        deps = self._deps(reads, writes)
        self._wait(e, deps, skip_self=(e == "pe"))
        if self.cnt[e] >= self.SEM_ROT and not self.pend.get(e, False):
            self.rot[e] += 1
            self.sem[e] = self.nc.alloc_semaphore(name=f"s_{e}_{self.rot[e]}")
            self.cnt[e] = 0
        ins = fn(self.eng[e])
        if inc:
            self.cnt[e] += 1
            ins.then_inc(self.sem[e], 1)
            tok = (self.sem[e], self.cnt[e])
            self.pend[e] = False
        else:
            tok = (self.sem[e], self.cnt[e] + 1)
            self.pend[e] = True
        self._record(tok, reads, writes)
        return tok

    def dma(self, q, out, in_, reads=(), writes=()):
        self.nops += 1
        if self.nops > self.limit:
            return None
        deps = self._deps(reads, writes)
        i = self.dnext[q] + (self.nd if q == "pool" else 0)
        self.dnext[q] = (self.dnext[q] + 1) % self.nd
        if self.dcnt[i] > 0:
            deps.append((self.dsems[i], self.dcnt[i]))
        self._wait(q, deps)
        ins = self.eng[q].dma_start(out=out, in_=in_)
        self.dcnt[i] += 16
        ins.then_inc(self.dsems[i], 16)
        tok = (self.dsems[i], self.dcnt[i])
        self._record(tok, reads, writes)
        return tok

    def allgather(self, groups, in_ap, out_ap, reads=(), writes=()):
        deps = self._deps(reads, writes)
        self._wait("pool", deps)
        ins = self.nc.gpsimd.collective_compute(
            "AllGather", ALU.bypass, replica_groups=groups,
            ins=[in_ap], outs=[out_ap])
        self.cccnt += 1
        ins.then_inc(self.ccsem, 1)
        tok = (self.ccsem, self.cccnt)
        self._record(tok, reads, writes)
        self._wait("pool", [tok])
        return tok

    def mark(self, name):
        self.log.append((name, self.nops))

    def barrier(self):
        deps = [(self.sem[e], self.cnt[e]) for e in self.eng if self.cnt[e] > 0]
        deps += [(s, c) for s, c in zip(self.dsems, self.dcnt) if c > 0]
        if self.cccnt:
            deps.append((self.ccsem, self.cccnt))
        for e in self.eng:
            self._wait(e, deps)


def build(cfg, debug=False, stop=99):
    c = cfg
    D, T, TO, TB, KD, KF, NT = c.D, c.T, c.TO, c.TB, c.KD, c.KF, c.NT
    HH, HF, HM, ML = c.HH, c.HF, c.HM, c.ML
    nc = bass.Bass("TRN2", target_bir_lowering=False)

    def din(name, shape, dt=F32):
        return nc.dram_tensor(name, list(shape), dt, kind="ExternalInput").ap()

    x_in = din("x", [TO, D])
    mem_in = din("mem", [ML, D])
    gains_in = din("gains", [7, D])
    ffw = []
    for f in (1, 2):
        ffw.append((din(f"f{f}_wg", [KF, 128, KD * 128]), din(f"f{f}_wu", [KF, 128, KD * 128]),
                    din(f"f{f}_wd", [KF, 128, D])))
    w_in_d = din("w_in", [D, c.NCOL])
    w_ff_d = din("w_ff", [D, HF])
    lbT_d = din("lbT", [128, HH * 2])
    gnT_d = din("gnT", [128, HH])
    fb_d = din("fb", [HF, 1])
    w_mk_d = din("w_mk", [D, HM * 128])
    w_mv_d = din("w_mv", [D, HM * 128])
    w_outp_d = din("w_outp", [KD, 128, c.NH2 * 128])
    w_gate_d = din("w_gate", [KD, 128, 3 * KD * 128])
    w_o_d = din("w_o", [KD, 128, D])
    sel_d = din("sel", [128, 2])
    out_d = nc.dram_tensor("out", [TO, D], F32, kind="ExternalOutput").ap()

    def dscr(name, shape, dt):
        if debug:
            return T_(nc.dram_tensor(name, list(shape), dt, kind="ExternalOutput").ap(), name)
        return T_(nc.dram_tensor(name, list(shape), dt).ap(), name)

    x1_s = dscr("x1_s", [TO, D], F32)
    x2_s = dscr("x2_s", [TO, D], F32)
    uT_own = T_(nc.dram_tensor("uT_own", [c.NBLK, D, TB], BF16).ap(), "uT_own")
    uT_all = T_(nc.dram_tensor("uT_all", [c.NBLK, 2 * D, TB], BF16).ap(), "uT_all")
    o_all = T_(nc.dram_tensor("o_all", [c.NBLK2, c.OW, TB], BF16).ap(), "o_all")
    o_gath = T_(nc.dram_tensor("o_gath", [c.NBLK2, 2 * c.OW, TB], BF16).ap(), "o_gath")
    fq_s = dscr("fq_s", [HF, 128, T], BF16)
    fk_s = dscr("fk_s", [HF, 128, T], BF16)
    fv_s = dscr("fv_s", [HF, 128, c.NTT, 128], BF16)
    mq_s = dscr("mq_s", [HM, 128, T], BF16)
    c3_s = dscr("c3_s", [HF, 3, T], BF16)
    out_t = T_(out_d, "out")
    groups = [[2 * i, 2 * i + 1] for i in range(c.NB)]

    kb = KB(nc)
    glob = ExitStack()

    uniq = [0]

    def mk(stack, kind):
        def f(name, shape, dt):
            fn = nc.sbuf_tensor if kind == "sb" else nc.psum_tensor
            uniq[0] += 1
            return T_(stack.enter_context(fn(f"{name}_{uniq[0]}", list(shape), dt)), name, excl=(kind == "ps"))
        return f

    gsb = mk(glob, "sb")
    gps = mk(glob, "ps")
    B = [gps(f"bank{i}", [128, 512], F32) for i in range(8)]

    ident = gsb("ident", [128, 128], F32)
    ones32 = gsb("ones32", [128, 128], F32)
    onesbf = gsb("onesbf", [128, 128], BF16)
    M2 = gsb("M2", [128, 128], F32)
    epsc = gsb("epsc", [128, 1], F32)
    onec = gsb("onec", [128, 1], F32)
    resetm = gsb("resetm", [128, TB], F32)
    scanone = gsb("scanone", [128, TB], F32)
    selc = gsb("selc", [128, 2], F32)
    negcT = gsb("negcT", [128, c.NTT, HF], F32)

    kb.op("pool", lambda e: e.memset(ident[:], 0.0), writes=[ident])
    kb.op("pool", lambda e: e.affine_select(out=ident[:], in_=ident[:], compare_op=ALU.not_equal,
                                            fill=1.0, base=0, pattern=[[-1, 128]], channel_multiplier=1),
          reads=[ident], writes=[ident])
    kb.op("pool", lambda e: e.memset(ones32[:], 1.0), writes=[ones32])
    kb.op("pool", lambda e: e.memset(onesbf[:], 1.0), writes=[onesbf])
    kb.op("pool", lambda e: e.memset(M2[:], 1.0), writes=[M2])
    kb.op("pool", lambda e: e.affine_select(out=M2[:], in_=M2[:], compare_op=ALU.is_ge, fill=0.0,
                                            base=0, pattern=[[1, 128]], channel_multiplier=-1),
          reads=[M2], writes=[M2])
    kb.op("pool", lambda e: e.memset(M2[0:64, 64:128], 0.0), reads=[M2], writes=[M2])
    kb.op("pool", lambda e: e.memset(epsc[:], EPS), writes=[epsc])
    kb.op("pool", lambda e: e.memset(onec[:], 1.0), writes=[onec])
    kb.op("pool", lambda e: e.memset(resetm[:], 1.0), writes=[resetm])
    kb.op("pool", lambda e: e.memset(scanone[:], 1.0), writes=[scanone])
    for j in range(TB // 64):
        kb.op("pool", lambda e, j=j: e.memset(resetm[:, 64 * j:64 * j + 1], 0.0), reads=[resetm], writes=[resetm])
    kb.dma("sp", selc[:], sel_d, writes=[selc])
    G_F1PRE, G_F1POST, G_MIXPRE, G_MIXPOST, G_MEM, G_F2PRE, G_F2POST = range(7)
    GAINS = {}

    def load_gains(stack, idxs):
        GAINS.clear()
        for g in idxs:
            t = mk(stack, "sb")(f"gain{g}", [128, D], F32)
            kb.dma("sp", t[:], gains_in[g:g + 1, :].partition_broadcast(128), writes=[t])
            if g in (G_F1POST, G_F2POST):
                kb.op("pool", lambda e, t=t: e.tensor_scalar(out=t[:], in0=t[:], scalar1=0.5, scalar2=None, op0=ALU.mult),
                      reads=[t], writes=[t])
            GAINS[g] = t

    def rstd_from_ss(ss, rs, n):
        kb.op("act", lambda e: e.activation(out=rs[:], in_=ss[:], func=AF.Sqrt, bias=epsc[:], scale=1.0 / n),
              reads=[ss, epsc], writes=[rs])
        kb.op("dve", lambda e: e.reciprocal(out=rs[:], in_=rs[:]), reads=[rs], writes=[rs])

    def norm_T(src_tile, gidx, dstT, col0, junk, ss, rs, xn, tb_banks, ntok=128):
        kb.op("act", lambda e: e.activation(out=junk[:ntok, :], in_=src_tile[:ntok, :], func=AF.Square,
                                            accum_out=ss[:ntok, :]),
              reads=[src_tile], writes=[junk, ss])
        rstd_from_ss(ss, rs, D)
        kb.op("dve", lambda e: e.scalar_tensor_tensor(out=xn[:ntok, :], in0=src_tile[:ntok, :], scalar=rs[:ntok, :],
                                                      in1=GAINS[gidx][:ntok, :], op0=ALU.mult, op1=ALU.mult),
              reads=[src_tile, rs, GAINS[gidx]], writes=[xn])
        KG = min(4, KD)
        for k0 in range(0, KD, KG):
            bk = tb_banks[(k0 // KG) % len(tb_banks)]
            for k in range(k0, k0 + KG):
                kb.op("pe", lambda e, k=k, bk=bk: e.transpose(out=bk[:, (k - k0) * 128:(k - k0) * 128 + ntok],
                                                             in_=xn[:ntok, k * 128:(k + 1) * 128],
                                                             identity=ident[:ntok, :ntok]),
                      reads=[xn, ident], writes=[bk], inc=(k == k0 + KG - 1))
            src = bk.ap[:, 0:KG * 128].rearrange("p (k t) -> p k t", k=KG)[:, :, 0:ntok]
            kb.op("act", lambda e, k0=k0, src=src: e.activation(out=dstT[:, k0:k0 + KG, col0:col0 + ntok], in_=src,
                                                               func=AF.Copy),
                  reads=[bk], writes=[dstT])

    def down_norm_res(stack_sb, hT, KC, w_dram, w_cast, gidx, xr_t, ot_t, tag=""):
        sbl = mk(stack_sb, "sb")
        NQP, NPASS, QW = c.NQP, c.NPASS, c.QW
        PWD = NQP * QW
        wslots = [sbl(f"wd{tag}{i}", [128, PWD], BF16) for i in range(4)]
        y_sb = sbl(f"ysb{tag}", [128, NT, D], F32)
        ssq = sbl(f"ssq{tag}", [128, NT * c.NQ, 8], F32)
        junk = sbl(f"djunk{tag}", [128, QW], BF16)
        ss = sbl(f"dss{tag}", [128, 1], F32)
        rs = sbl(f"drs{tag}", [128, 1], F32)
        if xr_t is None:
            xr_t = sbl(f"xres{tag}", [128, D], F32)
            ot_t = sbl(f"otile{tag}", [128, D], F32)
        st = {"n": 0}

        def run(hT, res_ap, res_t, dst_ap, dst_t, extra_fn=None):
            for p in range(NPASS):
                for kc in range(KC):
                    ws = wslots[st["n"] % 4]
                    st["n"] += 1
                    kb.dma("pool" if w_cast else "sp", ws[:], w_dram[kc, :, p * PWD:(p + 1) * PWD], writes=[ws])
                    for t in range(NT):
                        for q in range(NQP):
                            bk = B[t * NQP + q]
                            kb.op("pe", lambda e, bk=bk, t=t, q=q, ws=ws, kc=kc: e.matmul(
                                bk[:, 0:QW], lhsT=hT[:, kc, t * 128:(t + 1) * 128], rhs=ws[:, q * QW:(q + 1) * QW],
                                start=(kc == 0), stop=(kc == KC - 1)),
                                reads=[hT, ws], writes=[bk], inc=(kc == KC - 1 or (t == NT - 1 and q == NQP - 1)))
                for t in range(NT):
                    for q in range(NQP):
                        bk = B[t * NQP + q]
                        qq = p * NQP + q
                        kb.op("act", lambda e, bk=bk, t=t, qq=qq: e.activation(
                            out=junk[:, 0:QW], in_=bk[:, 0:QW], func=AF.Square, accum_out=ssq[:, t * c.NQ + qq, 0:1]),
                            reads=[bk], writes=[junk, ssq])
                        kb.op("dve", lambda e, bk=bk, t=t, qq=qq: e.tensor_copy(
                            out=y_sb[:, t, qq * QW:(qq + 1) * QW], in_=bk[:, 0:QW]),
                            reads=[bk], writes=[y_sb])
            kb.mark("down_epilogue")
            for t in range(NT):
                xr = xr_t
                ot = ot_t
                kb.dma("sp", xr[:], res_ap[t * 128:(t + 1) * 128, :], reads=[res_t], writes=[xr])
                kb.op("dve", lambda e, t=t: e.tensor_reduce(out=ss[:], in_=ssq[:, t * c.NQ:(t + 1) * c.NQ, 0], axis=AX.X, op=ALU.add),
                      reads=[ssq], writes=[ss])
                rstd_from_ss(ss, rs, D)
                kb.op("dve", lambda e, t=t, ot=ot: e.scalar_tensor_tensor(
                    out=ot[:], in0=y_sb[:, t, :], scalar=rs[:], in1=GAINS[gidx][:], op0=ALU.mult, op1=ALU.mult),
                    reads=[y_sb, rs, GAINS[gidx]], writes=[ot])
                kb.op("pool", lambda e, ot=ot, xr=xr: e.tensor_tensor(out=ot[:], in0=ot[:], in1=xr[:], op=ALU.add),
                      reads=[ot, xr], writes=[ot])
                kb.dma("sp", dst_ap[t * 128:(t + 1) * 128, :], ot[:], reads=[ot], writes=[dst_t])
                if extra_fn is not None:
                    extra_fn(t, ot)
        return run

    def ffn_phase(fi, src_ap, src_t, dst_ap, dst_t, g_pre, g_post, emit_u):
        wg_d, wu_d, wd_d = ffw[fi]
        with ExitStack() as ph:
            sbl = mk(ph, "sb")
            xt = [sbl(f"xt{i}", [128, D], F32) for i in range(2)]
            xn = sbl("xn", [128, D], F32)
            load_gains(ph, [g_pre, g_post] + ([G_MIXPRE] if emit_u else []))
            junk = sbl("junk", [128, D], BF16)
            ss = sbl("ss", [128, 1], F32)
            rs = sbl("rs", [128, 1], F32)
            xnT = sbl("xnT", [128, KD, TB], BF16)
            hT = sbl("hT", [128, KF, TB], BF16)
            NWS = 2
            wgs = [sbl(f"wgs{i}", [128, KD * 128], BF16) for i in range(NWS)]
            wus = [sbl(f"wus{i}", [128, KD * 128], BF16) for i in range(NWS)]
            sil = [sbl(f"sil{i}", [128, TB], F32) for i in range(2)]
            uTb = sbl("uTblk", [128, KD, 128], BF16) if emit_u else None
            run_down = down_norm_res(ph, hT, KF, wd_d, True, g_post, xt[0], xt[1], tag=f"f{fi}")
            for blk in range(c.NBLK):
                r0 = blk * TB
                kb.mark(f"ffn{fi}_blk{blk}_start")
                for t in range(NT):
                    xs = xt[t % 2]
                    kb.dma("sp", xs[:], src_ap[r0 + t * 128:r0 + (t + 1) * 128, :], reads=[src_t], writes=[xs])
                    norm_T(xs, g_pre, xnT, t * 128, junk, ss, rs, xn, [B[4], B[5]])
                kb.mark(f"ffn{fi}_blk{blk}_up")
                for kc in range(KF):
                    wg, wu = wgs[kc % NWS], wus[kc % NWS]
                    kb.dma("pool", wg[:], wg_d[kc], writes=[wg])
                    kb.dma("pool", wu[:], wu_d[kc], writes=[wu])
                    bg, bu = B[2 * (kc % 2)], B[2 * (kc % 2) + 1]
                    for (w, bk) in ((wg, bg), (wu, bu)):
                        for k in range(KD):
                            kb.op("pe", lambda e, w=w, bk=bk, k=k: e.matmul(
                                bk[:, 0:TB], lhsT=w[:, k * 128:(k + 1) * 128], rhs=xnT[:, k, :],
                                start=(k == 0), stop=(k == KD - 1)),
                                reads=[w, xnT], writes=[bk], inc=(k == KD - 1))
                    sl = sil[kc % 2]
                    kb.op("act", lambda e, sl=sl, bg=bg: e.activation(out=sl[:], in_=bg[:, 0:TB], func=AF.Silu),
                          reads=[bg], writes=[sl])
                    kb.op("dve", lambda e, sl=sl, bu=bu, kc=kc: e.tensor_tensor(
                        out=hT[:, kc, :], in0=sl[:], in1=bu[:, 0:TB], op=ALU.mult),
                        reads=[sl, bu], writes=[hT])

                def extra(t, ot, blk=blk):
                    norm_T(ot, G_MIXPRE, uTb, 0, junk, ss, rs, xn, [B[6], B[7]])
                    kb.dma("sp", uT_own.ap[blk].rearrange("(k p) t -> p k t", p=128)[:, :, t * 128:(t + 1) * 128],
                           uTb[:], reads=[uTb], writes=[uT_own])
                kb.mark(f"ffn{fi}_blk{blk}_down")
                run_down(hT, src_ap[r0:r0 + TB, :], src_t, dst_ap[r0:r0 + TB, :], dst_t,
                         extra if emit_u else None)
                kb.mark(f"ffn{fi}_blk{blk}_end")
        kb.barrier()

    def mixer_phase():
        with ExitStack() as ph:
            sbl = mk(ph, "sb")
            Win = sbl("Win", [128, KD, c.NCOL], BF16)
            Wff = sbl("Wff", [128, KD, HF], BF16)
            lbt = sbl("lbt", [128, HH, 2], F32)
            lb = sbl("lb", [128, HH], F32)
            oml = sbl("oml", [128, HH], F32)
            gn = sbl("gn", [128, HH], F32)
            fbn = sbl("fbn", [HF, 1], F32)
            kb.dma("pool", Win[:], w_in_d.rearrange("(k p) n -> p k n", p=128), writes=[Win])
            with nc.allow_non_contiguous_dma(reason="tiny"):
                kb.dma("pool", Wff[:], w_ff_d.rearrange("(k p) n -> p k n", p=128), writes=[Wff])
            kb.dma("sp", lbt[:], lbT_d.rearrange("p (h two) -> p h two", two=2), writes=[lbt])
            kb.dma("sp", gn[:], gnT_d, writes=[gn])
            kb.dma("sp", fbn[:], fb_d, writes=[fbn])
            kb.op("dve", lambda e: e.tensor_tensor(out=lb[:], in0=lbt[:, :, 0], in1=lbt[:, :, 1], op=ALU.subtract),
                  reads=[lbt], writes=[lb])
            kb.op("act", lambda e: e.activation(out=lb[:], in_=lb[:], func=AF.Sigmoid), reads=[lb], writes=[lb])
            kb.op("dve", lambda e: e.tensor_scalar(out=oml[:], in0=lb[:], scalar1=-1.0, scalar2=1.0,
                                                   op0=ALU.mult, op1=ALU.add), reads=[lb], writes=[oml])
            kb.op("dve", lambda e: e.tensor_scalar(out=fbn[:], in0=fbn[:], scalar1=-1.0, scalar2=None, op0=ALU.mult),
                  reads=[fbn], writes=[fbn])

            uTb = [sbl(f"uTb{i}", [128, KD, TB], BF16) for i in range(2)]
            NW = 1

            def wset(i):
                d = {}
                for nm in ("W1", "G", "BC", "EB", "ENB", "W2", "W3", "K32", "KTE", "OG", "ON"):
                    d[nm] = sbl(f"{nm}_{i}", [128, TB], F32)
                d["QT"] = sbl(f"QT_{i}", [128, TB], BF16)
                d["KT"] = sbl(f"KT_{i}", [128, TB], BF16)
                d["OB"] = sbl(f"OB_{i}", [128, TB], BF16)
                return d
            ws = [wset(i) for i in range(NW)]
            SM = [sbl(f"SM{i}", [128, 128], BF16) for i in range(2)]
            KTET = [sbl(f"KTET{i}", [128, 128], BF16) for i in range(2)]
            Vh = sbl("Vh", [128, NT, HH * 128], BF16)
            Vf = sbl("Vf", [128, NT, HF * 128], BF16)
            S32 = [sbl(f"S32_{l}", [128, 128], F32) for l in range(HH)]
            Sbf = [sbl(f"Sbf_{l}", [128, 128], BF16) for l in range(HH)]
            ccar = sbl("ccar", [HF, 1], F32)
            cblk = sbl("cblk", [HF, TB], F32)
            cw1 = sbl("cw1", [HF, TB], F32)
            cw2 = sbl("cw2", [HF, TB], F32)
            c3t = [sbl(f"c3t{i}", [HF, TB], BF16) for i in range(3)]
            fqb = [sbl(f"fqb{i}", [128, TB], BF16) for i in range(2)]
            for l in range(HH):
                kb.op("pool", lambda e, l=l: e.memset(S32[l][:], 0.0), writes=[S32[l]])
                kb.op("pool", lambda e, l=l: e.memset(Sbf[l][:], 0.0), writes=[Sbf[l]])
            kb.op("pool", lambda e: e.memset(ccar[:], 0.0), writes=[ccar])

            c_hq, c_hf, c_hi, c_hog = 0, HH * 128, 2 * HH * 128, 3 * HH * 128
            c_fq = 4 * HH * 128
            c_fk = c_fq + HF * 128
            c_fv = c_fk + HF * 128
            c_mq = c_fv + HF * 128
            qscale = 128.0 ** -0.5
            nfq = [0]

            for bi in range(c.NBLK2):
                r, lblk = bi // c.NBLK, bi % c.NBLK
                ub = uTb[bi % 2]
                kb.dma("sp", ub[:], uT_all.ap[lblk, r * D:(r + 1) * D, :].rearrange("(k p) t -> p k t", p=128),
                       reads=[uT_all], writes=[ub])
                g0 = bi * TB

                def fm_proj(bk, col0, ncols=128, Wt=None):
                    Wt_ = Win if Wt is None else Wt
                    for k in range(KD):
                        kb.op("pe", lambda e, k=k: e.matmul(bk[0:ncols, 0:TB], lhsT=Wt_[:, k, col0:col0 + ncols],
                                                            rhs=ub[:, k, :], start=(k == 0), stop=(k == KD - 1)),
                              reads=[Wt_, ub], writes=[bk], inc=(k == KD - 1))

                def tm_proj(col0, ncols, dstV):
                    for t in range(NT):
                        bk = B[3]
                        for k in range(KD):
                            kb.op("pe", lambda e, k=k, t=t: e.matmul(
                                bk[:, 0:ncols], lhsT=ub[:, k, t * 128:(t + 1) * 128], rhs=Win[:, k, col0:col0 + ncols],
                                start=(k == 0), stop=(k == KD - 1)),
                                reads=[Win, ub], writes=[bk], inc=(k == KD - 1))
                        kb.op("act", lambda e, t=t: e.activation(out=dstV[:, t, :], in_=bk[:, 0:ncols], func=AF.Copy),
                              reads=[bk], writes=[dstV])

                tm_proj(c_hi, HH * 128, Vh)
                tm_proj(c_fv, HF * 128, Vf)
                for l in range(HF):
                    kb.dma("sp", fv_s.ap[l, :, bi * NT:(bi + 1) * NT, :], Vf[:, :, l * 128:(l + 1) * 128],
                           reads=[Vf], writes=[fv_s])

                for l in range(HH):
                    w = ws[l % NW]
                    fm_proj(B[0], c_hq + l * 128)
                    fm_proj(B[1], c_hf + l * 128)
                    fm_proj(B[2], c_hog + l * 128)
                    kb.op("act", lambda e: e.activation(out=w["W1"][:], in_=B[1][:, 0:TB], func=AF.Sigmoid),
                          reads=[B[1]], writes=[w["W1"]])
                    kb.op("dve", lambda e, l=l: e.tensor_scalar(out=w["G"][:], in0=w["W1"][:], scalar1=oml[:, l:l + 1],
                                                                scalar2=lb[:, l:l + 1], op0=ALU.mult, op1=ALU.add),
                          reads=[w["W1"], oml, lb], writes=[w["G"]])
                    kb.op("act", lambda e: e.activation(out=w["W1"][:], in_=w["G"][:], func=AF.Ln),
                          reads=[w["G"]], writes=[w["W1"]])
                    kb.op("dve", lambda e: e.tensor_tensor_scan(out=w["BC"][:], data0=resetm[:], data1=w["W1"][:],
                                                                initial=0.0, op0=ALU.mult, op1=ALU.add),
                          reads=[resetm, w["W1"]], writes=[w["BC"]])
                    kb.op("act", lambda e: e.activation(out=w["EB"][:], in_=w["BC"][:], func=AF.Exp),
                          reads=[w["BC"]], writes=[w["EB"]])
                    kb.op("act", lambda e: e.activation(out=w["ENB"][:], in_=w["BC"][:], func=AF.Exp, scale=-1.0),
                          reads=[w["BC"]], writes=[w["ENB"]])
                    kb.op("act", lambda e: e.activation(out=w["W2"][:], in_=B[0][:, 0:TB], func=AF.Silu),
                          reads=[B[0]], writes=[w["W2"]])
                    kb.op("dve", lambda e: e.tensor_tensor(out=w["QT"][:], in0=w["W2"][:], in1=w["EB"][:], op=ALU.mult),
                          reads=[w["W2"], w["EB"]], writes=[w["QT"]])
                    kb.op("dve", lambda e: e.tensor_scalar(out=w["W3"][:], in0=w["G"][:], scalar1=-1.0, scalar2=1.0,
                                                           op0=ALU.mult, op1=ALU.add), reads=[w["G"]], writes=[w["W3"]])
                    kb.op("dve", lambda e: e.tensor_tensor(out=w["K32"][:], in0=w["W3"][:], in1=w["ENB"][:], op=ALU.mult),
                          reads=[w["W3"], w["ENB"]], writes=[w["K32"]])
                    kb.op("pool", lambda e: e.tensor_copy(out=w["KT"][:], in_=w["K32"][:]),
                          reads=[w["K32"]], writes=[w["KT"]])
                    for ci in range(TB // 64):
                        kb.op("dve", lambda e, ci=ci: e.tensor_scalar(
                            out=w["KTE"][:, 64 * ci:64 * ci + 64], in0=w["K32"][:, 64 * ci:64 * ci + 64],
                            scalar1=w["EB"][:, 64 * ci + 63:64 * ci + 64], scalar2=None, op0=ALU.mult),
                            reads=[w["K32"], w["EB"]], writes=[w["KTE"]])
                    kb.op("act", lambda e: e.activation(out=w["OG"][:], in_=B[2][:, 0:TB], func=AF.Silu),
                          reads=[B[2]], writes=[w["OG"]])
                    bo = B[4]
                    for t in range(NT):
                        sm, ktet = SM[t % 2], KTET[t % 2]
                        tc = slice(t * 128, (t + 1) * 128)
                        kb.op("pe", lambda e, tc=tc: e.matmul(B[5][:, 0:128], lhsT=w["KT"][:, tc], rhs=w["QT"][:, tc],
                                                              start=True, stop=True),
                              reads=[w["KT"], w["QT"]], writes=[B[5]])
                        kb.op("dve", lambda e, sm=sm: e.tensor_tensor(out=sm[:], in0=B[5][:, 0:128], in1=M2[:], op=ALU.mult),
                              reads=[B[5], M2], writes=[sm])
                        kb.op("pe", lambda e, tc=tc: e.transpose(out=B[6][:, 0:128], in_=w["KTE"][:, tc], identity=ident[:]),
                              reads=[w["KTE"], ident], writes=[B[6]])
                        kb.op("act", lambda e, ktet=ktet: e.activation(out=ktet[:], in_=B[6][:, 0:128], func=AF.Copy),
                              reads=[B[6]], writes=[ktet])
                        for ci in range(2):
                            cs = slice(64 * ci, 64 * ci + 64)
                            cc = slice(t * 128 + 64 * ci, t * 128 + 64 * ci + 64)
                            vsl = Vh.ap[64 * ci:64 * ci + 64, t, l * 128:(l + 1) * 128]
                            kb.op("pe", lambda e, cc=cc, l=l: e.matmul(bo[:, cc], lhsT=Sbf[l][:], rhs=w["QT"][:, cc],
                                                                       start=True, stop=False),
                                  reads=[Sbf[l], w["QT"]], writes=[bo], inc=False)
                            kb.op("pe", lambda e, cc=cc, cs=cs, vsl=vsl, sm=sm: e.matmul(
                                bo[:, cc], lhsT=vsl, rhs=sm[cs, 64 * ci:64 * ci + 64], start=False, stop=True),
                                reads=[Vh, sm], writes=[bo])
                            kb.op("pe", lambda e, cs=cs, vsl=vsl, ktet=ktet: e.matmul(
                                B[7][:, 0:128], lhsT=ktet[cs, :], rhs=vsl, start=True, stop=True),
                                reads=[ktet, Vh], writes=[B[7]])
                            ec = t * 128 + 64 * ci + 63
                            kb.op("dve", lambda e, l=l, ec=ec: e.scalar_tensor_tensor(
                                out=S32[l][:], in0=S32[l][:], scalar=w["EB"][:, ec:ec + 1], in1=B[7][:, 0:128],
                                op0=ALU.mult, op1=ALU.add),
                                reads=[S32[l], w["EB"], B[7]], writes=[S32[l]])
                            kb.op("act", lambda e, l=l: e.activation(out=Sbf[l][:], in_=S32[l][:], func=AF.Copy),
                                  reads=[S32[l]], writes=[Sbf[l]])
                    kb.op("act", lambda e: e.activation(out=w["W2"][:], in_=bo[:, 0:TB], func=AF.Square),
                          reads=[bo], writes=[w["W2"]])
                    kb.op("pe", lambda e: e.matmul(B[5][:, 0:TB], lhsT=ones32[:], rhs=w["W2"][:], start=True, stop=True),
                          reads=[ones32, w["W2"]], writes=[B[5]])
                    kb.op("act", lambda e: e.activation(out=w["W3"][:], in_=B[5][:, 0:TB], func=AF.Sqrt,
                                                        bias=epsc[:], scale=1.0 / 128),
                          reads=[B[5], epsc], writes=[w["W3"]])
                    kb.op("dve", lambda e: e.reciprocal(out=w["W3"][:], in_=w["W3"][:]), reads=[w["W3"]], writes=[w["W3"]])
                    kb.op("dve", lambda e: e.tensor_tensor(out=w["ON"][:], in0=bo[:, 0:TB], in1=w["W3"][:], op=ALU.mult),
                          reads=[bo, w["W3"]], writes=[w["ON"]])
                    kb.op("dve", lambda e, l=l: e.scalar_tensor_tensor(
                        out=w["OB"][:], in0=w["ON"][:], scalar=gn[:, l:l + 1], in1=w["OG"][:], op0=ALU.mult, op1=ALU.mult),
                        reads=[w["ON"], gn, w["OG"]], writes=[w["OB"]])
                    kb.dma("sp", o_all.ap[bi, l * 128:(l + 1) * 128, :], w["OB"][:], reads=[w["OB"]], writes=[o_all])

                def stash(col0, dst_ap, dst_t, scale):
                    bk = B[nfq[0] % 2]
                    sbuf = fqb[nfq[0] % 2]
                    nfq[0] += 1
                    fm_proj(bk, col0)
                    kb.op("dve", lambda e: e.tensor_scalar(out=sbuf[:], in0=bk[:, 0:TB], scalar1=scale, scalar2=None,
                                                           op0=ALU.mult), reads=[bk], writes=[sbuf])
                    kb.dma("sp", dst_ap, sbuf[:], reads=[sbuf], writes=[dst_t])
                for l in range(HF):
                    stash(c_fq + l * 128, fq_s.ap[l, :, g0:g0 + TB], fq_s, qscale)
                    stash(c_fk + l * 128, fk_s.ap[l, :, g0:g0 + TB], fk_s, 1.0)
                for l in range(HM):
                    stash(c_mq + l * 128, mq_s.ap[l, :, g0:g0 + TB], mq_s, qscale)

                fm_proj(B[2], 0, ncols=HF, Wt=Wff)
                kb.op("act", lambda e: e.activation(out=cw1[:], in_=B[2][0:HF, 0:TB], func=AF.Exp,
                                                    bias=fbn[:], scale=-1.0),
                      reads=[B[2], fbn], writes=[cw1])
                kb.op("act", lambda e: e.activation(out=cw1[:], in_=cw1[:], func=AF.Ln, bias=onec[0:HF, :], scale=1.0),
                      reads=[cw1, onec], writes=[cw1])
                kb.op("dve", lambda e: e.tensor_scalar(out=cw1[:], in0=cw1[:], scalar1=-1.0, scalar2=None, op0=ALU.mult),
                      reads=[cw1], writes=[cw1])
                kb.op("dve", lambda e: e.tensor_tensor_scan(out=cblk[:], data0=scanone[0:HF, :], data1=cw1[:],
                                                            initial=ccar[:], op0=ALU.mult, op1=ALU.add),
                      reads=[scanone, cw1, ccar], writes=[cblk])
                kb.op("act", lambda e: e.activation(out=ccar[:], in_=cblk[:, TB - 1:TB], func=AF.Copy),
                      reads=[cblk], writes=[ccar])
                kb.op("dve", lambda e: e.tensor_copy(out=c3t[0][:], in_=cblk[:]), reads=[cblk], writes=[c3t[0]])
                kb.op("dve", lambda e: e.tensor_tensor(out=cw1[:], in0=cblk[:], in1=c3t[0][:], op=ALU.subtract),
                      reads=[cblk, c3t[0]], writes=[cw1])
                kb.op("dve", lambda e: e.tensor_copy(out=c3t[1][:], in_=cw1[:]), reads=[cw1], writes=[c3t[1]])
                kb.op("dve", lambda e: e.tensor_tensor(out=cw2[:], in0=cw1[:], in1=c3t[1][:], op=ALU.subtract),
                      reads=[cw1, c3t[1]], writes=[cw2])
                kb.op("dve", lambda e: e.tensor_copy(out=c3t[2][:], in_=cw2[:]), reads=[cw2], writes=[c3t[2]])
                for i in range(3):
                    kb.dma("sp", c3_s.ap[:, i, g0:g0 + TB], c3t[i][:], reads=[c3t[i]], writes=[c3_s])
                for t in range(NT):
                    kb.op("pe", lambda e, t=t: e.transpose(out=B[6][:, 0:HF], in_=cblk[:, t * 128:(t + 1) * 128],
                                                           identity=ident[0:HF, 0:HF]),
                          reads=[cblk, ident], writes=[B[6]])
                    kb.op("dve", lambda e, t=t: e.tensor_scalar(out=negcT[:, bi * NT + t, :], in0=B[6][:, 0:HF],
                                                                scalar1=-1.0, scalar2=None, op0=ALU.mult),
                          reads=[B[6]], writes=[negcT])
        kb.barrier()

    def attn_phase():
        QB = TB
        NQB = T // QB
        KPQ = QB // 128
        with ExitStack() as ph:
            sbl = mk(ph, "sb")
            ones3 = sbl("ones3", [3, 128], BF16)
            kb.op("pool", lambda e: e.memset(ones3[:], 1.0), writes=[ones3])
            load_gains(ph, [G_MEM])
            KTh = [sbl(f"KTh{i}", [128, T], BF16) for i in range(1)] * 2
            QTh = [sbl(f"QTh{i}", [128, T], BF16) for i in range(1)] * 2
            Vhh = [sbl(f"Vhh{i}", [128, c.NTT, 128], BF16) for i in range(1)] * 2
            cr3 = [sbl(f"cr3{i}", [3, T], BF16) for i in range(1)] * 2
            Pt = [sbl(f"Pt{i}", [128, QB], BF16) for i in range(3)]
            rden = sbl("rden", [128, QB], F32)
            clampt = sbl("clampt", [128, QB], F32)
            osb = [sbl(f"osb{i}", [128, QB], BF16) for i in range(2)]
            memt = sbl("memt", [128, D], F32)
            mxn = sbl("mxn", [128, D], F32)
            mjunk = sbl("mjunk", [128, D], BF16)
            mss = sbl("mss", [128, 1], F32)
            mrs = sbl("mrs", [128, 1], F32)
            memnT = sbl("memnT", [128, KD, ML], BF16)
            Wmk = sbl("Wmk", [128, KD, HM * 128], BF16)
            Wmv = sbl("Wmv", [128, KD, HM * 128], BF16)
            mkT = [sbl(f"mkT{l}", [128, ML], BF16) for l in range(HM)]
            mV = sbl("mV", [128, ML // 128, HM * 128], BF16)
            kb.dma("pool", Wmk[:], w_mk_d.rearrange("(k p) n -> p k n", p=128), writes=[Wmk])
            kb.dma("pool", Wmv[:], w_mv_d.rearrange("(k p) n -> p k n", p=128), writes=[Wmv])
            for t in range(ML // 128):
                kb.dma("sp", memt[:], mem_in[t * 128:(t + 1) * 128, :], writes=[memt])
                norm_T(memt, G_MEM, memnT, t * 128, mjunk, mss, mrs, mxn, [B[6], B[7]])
            for l in range(HM):
                for k in range(KD):
                    kb.op("pe", lambda e, k=k, l=l: e.matmul(B[0][:, 0:ML], lhsT=Wmk[:, k, l * 128:(l + 1) * 128],
                                                             rhs=memnT[:, k, :], start=(k == 0), stop=(k == KD - 1)),
                          reads=[Wmk, memnT], writes=[B[0]], inc=(k == KD - 1))
                kb.op("act", lambda e, l=l: e.activation(out=mkT[l][:], in_=B[0][:, 0:ML], func=AF.Copy),
                      reads=[B[0]], writes=[mkT[l]])
            for t in range(ML // 128):
                for k in range(KD):
                    kb.op("pe", lambda e, k=k, t=t: e.matmul(B[1][:, 0:HM * 128], lhsT=memnT[:, k, t * 128:(t + 1) * 128],
                                                             rhs=Wmv[:, k, :], start=(k == 0), stop=(k == KD - 1)),
                          reads=[Wmv, memnT], writes=[B[1]], inc=(k == KD - 1))
                kb.op("act", lambda e, t=t: e.activation(out=mV[:, t, :], in_=B[1][:, 0:HM * 128], func=AF.Copy),
                      reads=[B[1]], writes=[mV])

            nb = [0]
            for hi_ in range(HF + HM):
                fox = hi_ < HF
                l = hi_ if fox else hi_ - HF
                s = hi_ % 2
                if fox:
                    kb.dma("sp", KTh[s][:], fk_s.ap[l], reads=[fk_s], writes=[KTh[s]])
                    kb.dma("sp", QTh[s][:], fq_s.ap[l], reads=[fq_s], writes=[QTh[s]])
                    kb.dma("sp", Vhh[s][:], fv_s.ap[l], reads=[fv_s], writes=[Vhh[s]])
                    kb.dma("sp", cr3[s][:], c3_s.ap[l], reads=[c3_s], writes=[cr3[s]])
                    orow = (HH + l) * 128
                else:
                    kb.dma("sp", QTh[s][:], mq_s.ap[l], reads=[mq_s], writes=[QTh[s]])
                    orow = (HH + HF + l) * 128
                for qi in range(NQB):
                    qc = slice(qi * QB, (qi + 1) * QB)
                    nkb = KPQ * (qi + 1) if fox else ML // 128
                    bo_, bd_ = B[4 + (qi % 2)], B[6 + (qi % 2)]

                    def s_mm(kbi):
                        bs = B[kbi % 3]
                        if fox:
                            kb.op("pe", lambda e: e.matmul(bs[:, 0:QB], lhsT=KTh[s][:, kbi * 128:(kbi + 1) * 128],
                                                           rhs=QTh[s][:, qc], start=True, stop=False),
                                  reads=[KTh[s], QTh[s]], writes=[bs], inc=False)
                            kb.op("pe", lambda e: e.matmul(bs[:, 0:QB], lhsT=ones3[:], rhs=cr3[s][:, qc],
                                                           start=False, stop=True),
                                  reads=[ones3, cr3[s]], writes=[bs])
                        else:
                            kb.op("pe", lambda e: e.matmul(bs[:, 0:QB], lhsT=mkT[l][:, kbi * 128:(kbi + 1) * 128],
                                                           rhs=QTh[s][:, qc], start=True, stop=True),
                                  reads=[mkT[l], QTh[s]], writes=[bs])

                    def rest(kbi):
                        bs = B[kbi % 3]
                        pt = Pt[nb[0] % 3]
                        nb[0] += 1
                        if fox:
                            d = kbi - KPQ * qi
                            if d >= 0:
                                kb.op("dve", lambda e: e.tensor_scalar(out=clampt[:], in0=bs[:, 0:QB],
                                                                       scalar1=negcT[:, kbi, l:l + 1], scalar2=30.0,
                                                                       op0=ALU.add, op1=ALU.min),
                                      reads=[bs, negcT], writes=[clampt])
                                kb.op("act", lambda e: e.activation(out=pt[:], in_=clampt[:], func=AF.Exp),
                                      reads=[clampt], writes=[pt])
                            else:
                                kb.op("act", lambda e: e.activation(out=pt[:], in_=bs[:, 0:QB], func=AF.Exp,
                                                                    bias=negcT[:, kbi, l:l + 1], scale=1.0),
                                      reads=[bs, negcT], writes=[pt])
                            if d >= 0:
                                kb.op("pool", lambda e: e.affine_select(
                                    out=pt[:], in_=pt[:], compare_op=ALU.is_ge, fill=0.0, base=-128 * d,
                                    pattern=[[1, QB]], channel_multiplier=-1), reads=[pt], writes=[pt])
                            vv = Vhh[s].ap[:, kbi, :]
                            vt = Vhh[s]
                        else:
                            kb.op("act", lambda e: e.activation(out=pt[:], in_=bs[:, 0:QB], func=AF.Exp),
                                  reads=[bs], writes=[pt])
                            vv = mV.ap[:, kbi, l * 128:(l + 1) * 128]
                            vt = mV
                        kb.op("pe", lambda e: e.matmul(bo_[:, 0:QB], lhsT=vv, rhs=pt[:], start=(kbi == 0),
                                                       stop=(kbi == nkb - 1)),
                              reads=[vt, pt], writes=[bo_], inc=False)
                        kb.op("pe", lambda e: e.matmul(bd_[:, 0:QB], lhsT=onesbf[:], rhs=pt[:], start=(kbi == 0),
                                                       stop=(kbi == nkb - 1)),
                              reads=[onesbf, pt], writes=[bd_])

                    s_mm(0)
                    for kbi in range(nkb):
                        if kbi + 1 < nkb:
                            s_mm(kbi + 1)
                        rest(kbi)
                    ob = osb[qi % 2]
                    kb.op("dve", lambda e: e.reciprocal(out=rden[:], in_=bd_[:, 0:QB]), reads=[bd_], writes=[rden])
                    kb.op("dve", lambda e: e.tensor_tensor(out=ob[:], in0=bo_[:, 0:QB], in1=rden[:], op=ALU.mult),
                          reads=[bo_, rden], writes=[ob])
                    kb.dma("sp", o_all.ap[qi, orow:orow + 128, :], ob[:], reads=[ob], writes=[o_all])
        kb.barrier()

    def merge_phase():
        NH2 = c.NH2
        nper = HH + HF + HM
        branch = []
        for r in range(2):
            branch += [0] * HH + [1] * HF + [2] * HM
        with ExitStack() as ph:
            sbl = mk(ph, "sb")
            load_gains(ph, [G_MIXPOST])
            oA = sbl("oA", [128, NH2, TB], BF16)
            oB = sbl("oB", [128, NH2, TB], BF16)
            ub = sbl("ubm", [128, KD, TB], BF16)
            mT = sbl("mergedT", [128, KD, TB], BF16)
            wgp = [sbl(f"wgp{i}", [128, 3, KD * 128], BF16) for i in range(2)]
            wop = [sbl(f"wop{i}", [128, NH2 * 128], BF16) for i in range(2)]
            sg = [sbl(f"sg{i}", [128, TB], F32) for i in range(3)]
            tt = [sbl(f"tt{i}", [128, TB], F32) for i in range(3)]
            run_down = down_norm_res(ph, mT, KD, w_o_d, True, G_MIXPOST, None, None, tag="mo")
            for blk in range(c.NBLK):
                r0 = blk * TB
                kb.dma("sp", oA[:], o_gath.ap[blk].rearrange("(h p) t -> p h t", p=128), reads=[o_gath], writes=[oA])
                kb.dma("sp", oB[:], o_gath.ap[c.NBLK + blk].rearrange("(h p) t -> p h t", p=128),
                       reads=[o_gath], writes=[oB])
                kb.dma("sp", ub[:], uT_own.ap[blk].rearrange("(k p) t -> p k t", p=128),
                       reads=[uT_own], writes=[ub])
                kb.op("dve", lambda e: e.tensor_scalar(out=oA[:], in0=oA[:], scalar1=selc[:, 0:1], scalar2=None,
                                                       op0=ALU.mult), reads=[oA, selc], writes=[oA])
                kb.op("dve", lambda e: e.scalar_tensor_tensor(out=oA[:], in0=oB[:], scalar=selc[:, 1:2], in1=oA[:],
                                                              op0=ALU.mult, op1=ALU.add),
                      reads=[oA, oB, selc], writes=[oA])
                for cc in range(KD):
                    wg_, wo_ = wgp[cc % 2], wop[cc % 2]
                    kb.dma("pool", wg_[:], w_gate_d[cc].rearrange("p (b n) -> p b n", b=3), writes=[wg_])
                    kb.dma("pool", wo_[:], w_outp_d[cc], writes=[wo_])
                    for br in range(3):
                        for k in range(KD):
                            kb.op("pe", lambda e, br=br, k=k: e.matmul(
                                B[br][:, 0:TB], lhsT=wg_[:, br, k * 128:(k + 1) * 128], rhs=ub[:, k, :],
                                start=(k == 0), stop=(k == KD - 1)),
                                reads=[wg_, ub], writes=[B[br]], inc=(k == KD - 1))
                    for br in range(3):
                        hs = [h for h in range(NH2) if branch[h] == br]
                        for i, h in enumerate(hs):
                            kb.op("pe", lambda e, br=br, h=h, i=i, hs=hs: e.matmul(
                                B[3 + br][:, 0:TB], lhsT=wo_[:, h * 128:(h + 1) * 128], rhs=oA[:, h, :],
                                start=(i == 0), stop=(i == len(hs) - 1)),
                                reads=[wo_, oA], writes=[B[3 + br]], inc=(i == len(hs) - 1))
                    for br in range(3):
                        kb.op("act", lambda e, br=br: e.activation(out=sg[br][:], in_=B[br][:, 0:TB], func=AF.Sigmoid),
                              reads=[B[br]], writes=[sg[br]])
                        kb.op("dve", lambda e, br=br: e.tensor_tensor(out=tt[br][:], in0=sg[br][:], in1=B[3 + br][:, 0:TB],
                                                                      op=ALU.mult),
                              reads=[sg[br], B[3 + br]], writes=[tt[br]])
                    kb.op("pool", lambda e: e.tensor_tensor(out=tt[0][:], in0=tt[0][:], in1=tt[1][:], op=ALU.add),
                          reads=[tt[0], tt[1]], writes=[tt[0]])
                    kb.op("pool", lambda e, cc=cc: e.tensor_tensor(out=mT[:, cc, :], in0=tt[0][:], in1=tt[2][:], op=ALU.add),
                          reads=[tt[0], tt[2]], writes=[mT])
                run_down(mT, x1_s.ap[r0:r0 + TB, :], x1_s, x2_s.ap[r0:r0 + TB, :], x2_s, None)
        kb.barrier()

    x_t = T_(x_in, "x")
    ffn_phase(0, x_in, x_t, x1_s.ap, x1_s, G_F1PRE, G_F1POST, True)
    if stop >= 2:
        for blk in range(c.NBLK):
            kb.allgather(groups, uT_own.ap[blk], uT_all.ap[blk], reads=[uT_own], writes=[uT_all])
        kb.barrier()
    if stop >= 3:
        mixer_phase()
    if stop >= 4:
        attn_phase()
    if stop >= 5:
        for blk in range(c.NBLK2):
            kb.allgather(groups, o_all.ap[blk], o_gath.ap[blk], reads=[o_all], writes=[o_gath])
        kb.barrier()
    if stop >= 6:
        merge_phase()
    if stop >= 7:
        ffn_phase(1, x2_s.ap, x2_s, out_d, out_t, G_F2PRE, G_F2POST, False)
    kb.barrier()
    glob.close()
    nc._kb_log = kb.log
    return nc


def prep_inputs(cfg, inp):
    c = cfg
    D, KD, KF, HH, HF, HM = c.D, c.KD, c.KF, c.HH, c.HF, c.HM
    f32 = np.float32

    def A(v):
        return np.ascontiguousarray(np.asarray(v, dtype=f32))

    def up_layout(w):
        return A(w.reshape(KD, 128, KF, 128).transpose(2, 1, 0, 3).reshape(KF, 128, KD * 128))

    gains = A(np.stack([inp["ffn1_pre"][0], inp["ffn1_post"][0], inp["mix_pre"][0], inp["mix_post"][0],
                        inp["mem_norm"][0], inp["ffn2_pre"][0], inp["ffn2_post"][0]]))
    shared = {
        "gains": gains,
        "f1_wg": up_layout(np.asarray(inp["ffn1_wg"][0])), "f1_wu": up_layout(np.asarray(inp["ffn1_wu"][0])),
        "f1_wd": A(np.asarray(inp["ffn1_wd"][0]).reshape(KF, 128, D)),
        "f2_wg": up_layout(np.asarray(inp["ffn2_wg"][0])), "f2_wu": up_layout(np.asarray(inp["ffn2_wu"][0])),
        "f2_wd": A(np.asarray(inp["ffn2_wd"][0]).reshape(KF, 128, D)),
        "w_o": A(np.asarray(inp["w_o"][0]).reshape(KD, 128, D)),
    }
    wgate = np.asarray(inp["w_gate"][0])
    shared["w_gate"] = A(wgate.reshape(KD, 128, 3, KD, 128).transpose(3, 1, 2, 0, 4).reshape(KD, 128, 3 * KD * 128))
    who, wfo, wmo = (np.asarray(inp[k][0]) for k in ("w_hgrn_out", "w_fox_out", "w_mem_out"))
    rows = []
    for r in range(2):
        for l in range(HH):
            rows.append(who[(r * HH + l) * 128:(r * HH + l + 1) * 128])
        for l in range(HF):
            rows.append(wfo[(r * HF + l) * 128:(r * HF + l + 1) * 128])
        for l in range(HM):
            rows.append(wmo[(r * HM + l) * 128:(r * HM + l + 1) * 128])
    wcat = np.stack(rows)
    shared["w_outp"] = A(wcat.reshape(c.NH2, 128, KD, 128).transpose(2, 1, 0, 3).reshape(KD, 128, c.NH2 * 128))

    w_in = np.asarray(inp["w_in"][0])
    WH, WF, WM = 2 * HH * 128, 2 * HF * 128, 2 * HM * 128
    offs = np.cumsum([0, WH, WH, WH, WH, WF, WF, WF, 2 * HF, WM])
    seg = [w_in[:, offs[i]:offs[i + 1]] for i in range(9)]
    lb = np.asarray(inp["hgrn_lb"])
    gnm = np.asarray(inp["hgrn_gnorm"][0])
    fb = np.asarray(inp["fox_fb"][0])
    wkv = np.asarray(inp["w_mem_kv"][0])
    x = np.asarray(inp["x"])
    mem = np.asarray(inp["mem"])
    maps = []
    for core in range(c.ncores):
        b, j = core // 2, core % 2
        m = dict(shared)
        m["x"] = A(x[b, j * c.TO:(j + 1) * c.TO])
        m["mem"] = A(mem[b])
        cols = []
        for i in range(4):
            cols.append(seg[i][:, j * HH * 128:(j + 1) * HH * 128])
        for i in range(4, 7):
            cols.append(seg[i][:, j * HF * 128:(j + 1) * HF * 128])
        cols.append(seg[8][:, j * HM * 128:(j + 1) * HM * 128])
        m["w_in"] = A(np.concatenate(cols, axis=1))
        m["w_ff"] = A(seg[7][:, j * HF:(j + 1) * HF])
        lbs = lb[:, j * HH * 128:(j + 1) * HH * 128].reshape(2, HH, 128)
        m["lbT"] = A(lbs.transpose(2, 1, 0).reshape(128, HH * 2))
        m["gnT"] = A(gnm[j * HH * 128:(j + 1) * HH * 128].reshape(HH, 128).T)
        m["fb"] = A(fb[j * HF:(j + 1) * HF].reshape(HF, 1))
        m["w_mk"] = A(wkv[:, j * HM * 128:(j + 1) * HM * 128])
        m["w_mv"] = A(wkv[:, 2 * HM * 128 + j * HM * 128: 2 * HM * 128 + (j + 1) * HM * 128])
        s = np.zeros((128, 2), f32)
        s[:, j] = 1.0
        m["sel"] = s
        maps.append(m)
    return maps


def kernel(**inputs):
    cfg = Cfg()
    nc = build(cfg)
    maps = prep_inputs(cfg, inputs)
    res = run_bass_kernel_spmd(nc, maps, core_ids=list(range(cfg.ncores)))
    out = np.empty((cfg.NB, cfg.T, cfg.D), np.float32)
    for core in range(cfg.ncores):
        b, j = core // 2, core % 2
        out[b, j * cfg.TO:(j + 1) * cfg.TO] = res.results[core]["out"]
    return out
```
